# Optimizing a Trainium2 kernel written in Bass

```python
import math
import jax, jax.numpy as jnp
from jax import lax
import numpy as np

D_MODEL = 2048
BATCH = 8
SEQ = 2048
DEPTH = 1

CHUNK = 64
Q_BLOCK = 128
MIX_WIDTH = D_MODEL
DIFF_HEADS = 4
DIFF_QK_DIM = 128
DIFF_V_DIM = 2 * DIFF_QK_DIM
DIFF_WIDTH = DIFF_HEADS * DIFF_V_DIM
GMLP_CHUNK = 128
GMLP_GROUPS = 8
GMLP_GROUP_DIM = 128
GMLP_WIDTH = GMLP_GROUPS * GMLP_GROUP_DIM
IN_DIM = 3 * DIFF_HEADS * 2 * DIFF_QK_DIM // 2 * 2 // 2 * 1 + 0
IN_DIM = 2 * (DIFF_HEADS * 2 * DIFF_QK_DIM) + DIFF_WIDTH + 2 * GMLP_WIDTH
NUM_BUCKETS = 32
MAX_DISTANCE = 128
FFN_HIDDEN = ((8 * D_MODEL // 3 + 255) // 256) * 256
PLE_DIM = 256
EPS = 1e-6
NEG_INF = -1e30

kernel_name = "hybrid_diffattn_gmlp_chunk_causal_block"


def rms_norm(x, g):
    x32 = x.astype(jnp.float32)
    y = x32 * lax.rsqrt(jnp.mean(x32 * x32, axis=-1, keepdims=True) + EPS)
    return (y * g.astype(jnp.float32)).astype(x.dtype)


def layer_norm(x, g, b):
    x32 = x.astype(jnp.float32)
    mu = jnp.mean(x32, axis=-1, keepdims=True)
    xc = x32 - mu
    y = xc * lax.rsqrt(jnp.mean(xc * xc, axis=-1, keepdims=True) + EPS)
    return (y * g.astype(jnp.float32) + b.astype(jnp.float32)).astype(x.dtype)


def t5_bucket(rel):
    half = NUM_BUCKETS // 2
    max_exact = half // 2
    ret = jnp.where(rel > 0, half, 0)
    n = jnp.abs(rel)
    nf = jnp.maximum(n, 1).astype(jnp.float32)
    large = max_exact + (jnp.log(nf / max_exact) / math.log(MAX_DISTANCE / max_exact)
                         * (half - max_exact)).astype(jnp.int32)
    large = jnp.minimum(large, half - 1)
    return ret + jnp.where(n < max_exact, n, large)


def diff_attention(q, k, v, lam, rel_bias):
    seq = q.shape[1]
    scale = DIFF_QK_DIM ** -0.5
    outs = []
    for blk in range(seq // Q_BLOCK):
        q0, q1 = blk * Q_BLOCK, (blk + 1) * Q_BLOCK
        qb, kb, vb = q[:, q0:q1], k[:, :q1], v[:, :q1]
        qpos = jnp.arange(q0, q1, dtype=jnp.int32)
        kpos = jnp.arange(q1, dtype=jnp.int32)
        logits = jnp.einsum('bqhmd,bkhmd->bhmqk', qb, kb).astype(jnp.float32) * scale
        bias = rel_bias[t5_bucket(kpos[None, :] - qpos[:, None])]
        bias = jnp.transpose(bias, (2, 0, 1)).astype(jnp.float32)
        mask = (kpos[None, :] // CHUNK) <= (qpos[:, None] // CHUNK)
        logits = jnp.where(mask, logits + bias[None, :, None], NEG_INF)
        probs = jax.nn.softmax(logits, axis=-1)
        attn = probs[:, :, 0] - lam * probs[:, :, 1]
        outs.append(jnp.einsum('bhqk,bkhe->bqhe', attn.astype(v.dtype), vb))
    return jnp.concatenate(outs, axis=1)


def gmlp_spatial_gate(u, vg, ln_g, ln_b, ws, bs):
    bsz, seq, _ = u.shape
    nb = seq // GMLP_CHUNK
    vn = layer_norm(vg, ln_g, ln_b).reshape(bsz, nb, GMLP_CHUNK, GMLP_GROUPS, GMLP_GROUP_DIM)
    pos = jnp.arange(GMLP_CHUNK)
    mask = (pos[None, :] // CHUNK) <= (pos[:, None] // CHUNK)
    ws_m = jnp.where(mask[None], ws, jnp.zeros_like(ws))
    s = jnp.einsum('gij,bnjgc->bnigc', ws_m, vn) + jnp.transpose(bs)[None, None, :, :, None]
    out = u.reshape(bsz, nb, GMLP_CHUNK, GMLP_GROUPS, GMLP_GROUP_DIM) * s
    return out.reshape(bsz, seq, GMLP_WIDTH)


def setup_inputs(seed: int = 0) -> dict:
    key = jax.random.key(seed)
    ks = jax.random.split(key, 24)
    f32 = jnp.float32
    nrm = lambda k, shape, s: jax.random.normal(k, shape, f32) * s
    gain = lambda k, shape: 1.0 + 0.05 * jax.random.normal(k, shape, f32)
    return {
        "x": jax.random.normal(ks[0], (BATCH, SEQ, D_MODEL), f32),
        "p": jax.random.normal(ks[1], (DEPTH, BATCH, SEQ, PLE_DIM), f32),
        "w_in": nrm(ks[2], (DEPTH, D_MODEL, IN_DIM), D_MODEL ** -0.5),
        "w_out": nrm(ks[3], (DEPTH, MIX_WIDTH, D_MODEL), MIX_WIDTH ** -0.5),
        "attn_norm_g": gain(ks[4], (DEPTH, D_MODEL)),
        "ffn_norm_g": gain(ks[5], (DEPTH, D_MODEL)),
        "final_norm_g": gain(ks[6], (D_MODEL,)),
        "lambda_q1": nrm(ks[7], (DEPTH, DIFF_QK_DIM), 0.1),
        "lambda_k1": nrm(ks[8], (DEPTH, DIFF_QK_DIM), 0.1),
        "lambda_q2": nrm(ks[9], (DEPTH, DIFF_QK_DIM), 0.1),
        "lambda_k2": nrm(ks[10], (DEPTH, DIFF_QK_DIM), 0.1),
        "subln_g": gain(ks[11], (DEPTH, DIFF_V_DIM)),
        "rel_bias": nrm(ks[12], (NUM_BUCKETS, DIFF_HEADS), 0.2),
        "gmlp_ln_g": gain(ks[13], (DEPTH, GMLP_WIDTH)),
        "gmlp_ln_b": nrm(ks[14], (DEPTH, GMLP_WIDTH), 0.02),
        "gmlp_ws": nrm(ks[15], (DEPTH, GMLP_GROUPS, GMLP_CHUNK, GMLP_CHUNK), GMLP_CHUNK ** -0.5),
        "gmlp_b": gain(ks[16], (DEPTH, GMLP_GROUPS, GMLP_CHUNK)),
        "ffn_w1": nrm(ks[17], (DEPTH, D_MODEL, FFN_HIDDEN), D_MODEL ** -0.5),
        "ffn_w3": nrm(ks[18], (DEPTH, D_MODEL, FFN_HIDDEN), D_MODEL ** -0.5),
        "ffn_w2": nrm(ks[19], (DEPTH, FFN_HIDDEN, D_MODEL), FFN_HIDDEN ** -0.5),
        "pl_w_up": nrm(ks[20], (DEPTH, PLE_DIM, D_MODEL), PLE_DIM ** -0.5),
        "pl_w_gate": nrm(ks[21], (DEPTH, D_MODEL, D_MODEL), D_MODEL ** -0.5),
    }


def reference(x, p, w_in, w_out, attn_norm_g, ffn_norm_g, final_norm_g,
              lambda_q1, lambda_k1, lambda_q2, lambda_k2, subln_g, rel_bias,
              gmlp_ln_g, gmlp_ln_b, gmlp_ws, gmlp_b, ffn_w1, ffn_w3, ffn_w2,
              pl_w_up, pl_w_gate):
    bsz, seq, _ = x.shape
    qk_cols = DIFF_HEADS * 2 * DIFF_QK_DIM
    splits = [qk_cols, 2 * qk_cols, 2 * qk_cols + DIFF_WIDTH,
              2 * qk_cols + DIFF_WIDTH + GMLP_WIDTH]
    h = x
    for i in range(DEPTH):
        a = rms_norm(h, attn_norm_g[i])
        z = a @ w_in[i]
        zq, zk, zv, zu, zg = jnp.split(z, splits, axis=-1)
        q = zq.reshape(bsz, seq, DIFF_HEADS, 2, DIFF_QK_DIM)
        k = zk.reshape(bsz, seq, DIFF_HEADS, 2, DIFF_QK_DIM)
        v = zv.reshape(bsz, seq, DIFF_HEADS, DIFF_V_DIM)
        lam_init = 0.8 - 0.6 * math.exp(-0.3 * i)
        lam = (jnp.exp(jnp.sum(lambda_q1[i].astype(jnp.float32) * lambda_k1[i].astype(jnp.float32)))
               - jnp.exp(jnp.sum(lambda_q2[i].astype(jnp.float32) * lambda_k2[i].astype(jnp.float32)))
               + lam_init)
        o_diff = diff_attention(q, k, v, lam, rel_bias)
        o_diff = rms_norm(o_diff, subln_g[i]) * (1.0 - lam_init)
        o_diff = o_diff.reshape(bsz, seq, DIFF_WIDTH)
        o_gmlp = gmlp_spatial_gate(jax.nn.gelu(zu), jax.nn.gelu(zg),
                                   gmlp_ln_g[i], gmlp_ln_b[i], gmlp_ws[i], gmlp_b[i])
        mixed = jnp.concatenate([o_diff.astype(h.dtype), o_gmlp.astype(h.dtype)], axis=-1)
        h = h + mixed @ w_out[i]
        f = rms_norm(h, ffn_norm_g[i])
        h = h + (jax.nn.silu(f @ ffn_w1[i]) * (f @ ffn_w3[i])) @ ffn_w2[i]
        h = h + (p[i] @ pl_w_up[i]) * jax.nn.sigmoid(h @ pl_w_gate[i])
    return rms_norm(h, final_norm_g)
```

```python
import math
from contextlib import ExitStack
import numpy as np
import concourse.bass as bass
import concourse.mybir as mybir
from concourse.bass_utils import run_bass_kernel_spmd
from concourse.alu_op_type import AluOpType as ALU

F32 = mybir.dt.float32
BF16 = mybir.dt.bfloat16
AF = mybir.ActivationFunctionType

S = 2048
D = 2048
NTB = 16
HID = 5632
NHC = 44
EPS = 1e-6
ENGS = ("sync", "gpsimd", "tensor", "vector", "scalar")
CFG = {"skipAB": False, "heads": 4, "nb": NTB, "near": True, "comb": 9, "pv": True, "ones": True}


class Op:
    __slots__ = ("eng", "fn", "deps", "dma", "sem", "val", "signal", "ss")

    def __init__(self, eng, fn, dma, ss=False):
        self.eng, self.fn, self.dma, self.ss = eng, fn, dma, ss
        self.deps, self.sem, self.val, self.signal = [], None, 0, False


class Prog:
    def __init__(self, nc, es):
        self.nc, self.es = nc, es
        self.esem = {e: es.enter_context(nc.semaphore("pg_" + e)) for e in ENGS}
        self.ecnt = {e: 0 for e in ENGS}
        self.dsem = {}
        self.dfree = []
        self.nsem = 0
        self.waited = {e: {} for e in ENGS}
        self.reset()

    def reset(self):
        self.ops = {e: [] for e in ENGS}
        self.lastw, self.rd_c, self.rd_d = {}, {}, {}

    def _dep(self, o, other, raw=False):
        if other is None or other is o:
            return
        if (not other.dma) and other.eng == o.eng and not (o.ss or (raw and o.eng != "tensor")):
            return
        if not other.dma:
            other.signal = True
        o.deps.append(other)

    def chain(self, eng, fns, reads=(), writes=()):
        o = None
        for fn in fns:
            o = self.op(eng, fn, reads=reads, writes=writes, ss=True)
        return o

    def op(self, eng, fn, reads=(), writes=(), dma=None, ss=False):
        o = Op(eng, fn, dma is not None, ss)
        for k in reads:
            self._dep(o, self.lastw.get(k), True)
        for k in writes:
            self._dep(o, self.lastw.get(k))
            for r in self.rd_c.get(k, {}).values():
                self._dep(o, r)
            for r in self.rd_d.get(k, []):
                self._dep(o, r)
        for k in writes:
            self.lastw[k] = o
            self.rd_c[k] = {}
            self.rd_d[k] = []
        for k in reads:
            if o.dma:
                self.rd_d.setdefault(k, []).append(o)
            else:
                self.rd_c.setdefault(k, {})[eng] = o
        if o.dma:
            if dma not in self.dsem:
                if self.dfree:
                    self.dsem[dma] = self.dfree.pop()
                else:
                    self.nsem += 1
                    self.dsem[dma] = [self.es.enter_context(self.nc.semaphore("d%d" % self.nsem)), 0]
            ent = self.dsem[dma]
            ent[1] += 16
            o.sem, o.val = ent[0], ent[1]
        self.ops[eng].append(o)
        return o

    def flush(self):
        for e in ENGS:
            for o in self.ops[e]:
                if o.signal and not o.dma:
                    self.ecnt[e] += 1
                    o.sem, o.val = self.esem[e], self.ecnt[e]
        ops, waited, esem = self.ops, self.waited, self.esem

        def run(e, eng):
            w = waited[e]
            outstanding = {}
            for o in ops[e]:
                need = {}
                for d in o.deps:
                    if need.get(d.sem, (None, 0))[1] < d.val:
                        need[d.sem] = (d.sem, d.val)
                for sem, val in need.values():
                    if w.get(sem, 0) < val:
                        eng.wait_ge(sem, val)
                        w[sem] = val
                ins = o.fn(eng)
                if o.dma:
                    ins.then_inc(o.sem, 16)
                    outstanding[o.sem] = (o.sem, o.val)
                elif o.signal:
                    ins.then_inc(esem[e], 1)
            for sem, val in outstanding.values():
                if w.get(sem, 0) < val:
                    eng.wait_ge(sem, val)
                    w[sem] = val

        with self.nc.Block() as block:
            @block.sync
            def _(eng):
                run("sync", eng)

            @block.gpsimd
            def _(eng):
                run("gpsimd", eng)

            @block.tensor
            def _(eng):
                run("tensor", eng)

            @block.vector
            def _(eng):
                run("vector", eng)

            @block.scalar
            def _(eng):
                run("scalar", eng)
        self.dfree.extend(self.dsem.values())
        self.dsem = {}
        self.reset()


def t5_bucket_np(rel):
    half, max_exact = 16, 8
    ret = np.where(rel > 0, half, 0)
    n = np.abs(rel)
    nf = np.maximum(n, 1).astype(np.float32)
    large = max_exact + (np.log(nf / np.float32(max_exact)) / np.float32(math.log(128 / max_exact))
                         * np.float32(half - max_exact)).astype(np.int32)
    large = np.minimum(large, half - 1)
    return ret + np.where(n < max_exact, n, large)


def build(stop_after=None):
    nc = bass.Bass("TRN2", target_bir_lowering=False)
    es = ExitStack()

    def din(name, shape, dt=F32):
        return nc.dram_tensor(name, shape, dt, kind="ExternalInput").ap()

    x = din("x", [S, D])
    p_in = din("p", [S, 256])
    w_in = din("w_in", [D, 5120])
    w_out = din("w_out", [D, D])
    w1 = din("ffn_w1", [D, HID])
    w3 = din("ffn_w3", [D, HID])
    w2 = din("ffn_w2", [HID, D])
    w_up = din("pl_w_up", [256, D])
    w_gate = din("pl_w_gate", [D, D])
    g1 = din("g1T", [128, 16])
    g2 = din("g2T", [128, 16])
    gfb = din("gfB", [128, D])
    lam4 = din("lam4", [128, 512])
    sublnb = din("sublnB", [128, 256])
    lngb = din("lngB", [128, 1024])
    lnbb = din("lnbB", [128, 1024])
    ws_in = din("ws", [8, 128, 128])
    bsT_in = din("bsT", [128, 8])
    bn_in = din("bnear", [128, 4 * 2 * 128])
    r15_in = din("r15", [128, 4])
    mk_in = din("mk", [128, 128])
    dbg = stop_after is not None
    skind = "ExternalOutput" if dbg else "Internal"
    out = nc.dram_tensor("out", [S, D], F32, kind="ExternalOutput").ap()
    QK = nc.dram_tensor("QK", [16, 128, S], BF16, kind=skind).ap()
    VS = nc.dram_tensor("VS", [S, 1024], BF16, kind=skind).ap()
    US = nc.dram_tensor("US", [S, 1024], BF16, kind=skind).ap()
    VN = nc.dram_tensor("VN", [S, 1024], BF16, kind=skind).ap()
    MX = nc.dram_tensor("MX", [S, D], BF16, kind=skind).ap()
    H1 = nc.dram_tensor("H1", [S, D], F32, kind=skind).ap()
    GT = nc.dram_tensor("GT", [NHC, 128, S], BF16, kind=skind).ap()
    H2 = nc.dram_tensor("H2", [S, D], F32, kind=skind).ap()
    DBG = nc.dram_tensor("DBG", [128, 2048], F32, kind=skind).ap()

    P = Prog(nc, es)
    sb = lambda name, shape, dt=F32: es.enter_context(nc.sbuf_tensor(name, shape, dt))

    ident = sb("ident", [128, 128], BF16)
    identf = sb("identf", [128, 128])
    onesf = sb("onesf", [128, 128])
    g1T = sb("g1T_s", [128, 16])
    g2T = sb("g2T_s", [128, 16])
    epsc = sb("epsc", [128, 1])
    lamt = sb("lamt", [128, 512])
    lamj = sb("lamj", [128, 128])
    lsum = sb("lsum", [128, 4])
    neglam = sb("neglam", [128, 1])
    bsT = sb("bsT_s", [128, 8])
    r15 = sb("r15_s", [128, 4])
    nr15 = sb("nr15", [128, 4])
    mk = sb("mk_s", [128, 128])
    EB = sb("EB", [128, 8, 128])
    wsf = sb("wsf", [128, 8, 128])
    wsb = sb("wsb", [128, 8, 128], BF16)
    wsT = sb("wsT", [128, 8, 128], BF16)
    small = sb("small", [128, 64])

    def cdma(dst, src, key):
        P.op("sync", lambda e: e.dma_start(out=dst, in_=src), writes=[key], dma=key)

    cdma(g1T[:], g1, "g1T")
    cdma(g2T[:], g2, "g2T")
    cdma(lamt[:], lam4, "lamt")
    cdma(bsT[:], bsT_in, "bsT")
    cdma(r15[:], r15_in, "r15")
    cdma(mk[:], mk_in, "mk")
    cdma(EB[:].rearrange("p a b -> p (a b)"), bn_in, "EB")
    cdma(wsf[:], ws_in.rearrange("g i j -> i g j"), "wsf")

    def mk_ident(e):
        e.memset(identf[:], 0.0)
        e.memset(onesf[:], 1.0)
        e.memset(epsc[:], EPS)
        return e.affine_select(out=identf[:], in_=onesf[:], pattern=[[1, 128]], compare_op=ALU.is_equal,
                               fill=0.0, base=0, channel_multiplier=-1)
    P.op("gpsimd", mk_ident, writes=["identf", "epsc"])
    P.op("vector", lambda e: e.tensor_copy(ident[:], identf[:]), reads=["identf"], writes=["ident"])
    P.op("gpsimd", lambda e: e.memset(wsf[0:64, :, 64:128], 0.0), reads=["wsf"], writes=["wsf"])
    P.op("vector", lambda e: e.tensor_copy(wsb[:], wsf[:]), reads=["wsf"], writes=["wsb"])
    P.op("vector", lambda e: e.tensor_tensor(out=lamt[:, 0:128], in0=lamt[:, 0:128], in1=lamt[:, 128:256], op=ALU.mult),
         reads=["lamt"], writes=["lamt"])
    P.op("vector", lambda e: e.tensor_tensor(out=lamt[:, 256:384], in0=lamt[:, 256:384], in1=lamt[:, 384:512], op=ALU.mult),
         reads=["lamt"], writes=["lamt"])

    P.chain("scalar", [
        lambda e: e.activation(out=lamj[:], in_=lamt[:, 0:128], func=AF.Copy, accum_out=lsum[:, 0:1]),
        lambda e: e.activation(out=lamj[:], in_=lamt[:, 256:384], func=AF.Copy, accum_out=lsum[:, 1:2]),
        lambda e: e.activation(out=lsum[:, 2:4], in_=lsum[:, 0:2], func=AF.Exp),
    ], reads=["lamt"], writes=["lsum"])
    P.chain("vector", [
        lambda e: e.tensor_tensor(out=neglam[:], in0=lsum[:, 3:4], in1=lsum[:, 2:3], op=ALU.subtract),
        lambda e: e.tensor_scalar(out=neglam[:], in0=neglam[:], scalar1=-0.2, scalar2=None, op0=ALU.add),
        lambda e: e.tensor_scalar(out=nr15[:], in0=r15[:], scalar1=-1.0, scalar2=None, op0=ALU.mult),
    ], reads=["lsum", "r15"], writes=["neglam", "nr15"])

    def eb_act(e):
        ins = None
        for h in range(4):
            ins = e.activation(out=EB[:, 2 * h:2 * h + 2, :], in_=EB[:, 2 * h:2 * h + 2, :], func=AF.Exp,
                               bias=nr15[:, h:h + 1])
        return ins
    P.op("scalar", eb_act, reads=["EB", "nr15"], writes=["EB"])

    def eb_mask(e):
        ins = None
        for h in range(4):
            ins = e.tensor_tensor(out=EB[:, 2 * h, :], in0=EB[:, 2 * h, :], in1=mk[:], op=ALU.mult)
        return ins
    P.op("vector", eb_mask, reads=["EB", "mk"], writes=["EB"])

    with nc.psum_tensor("ptw", [128, 8, 128], BF16) as ptw:
        def ws_tr(e):
            ins = None
            for g in range(8):
                ins = e.transpose(ptw[:, g, :], wsb[:, g, :], ident[:])
            return ins
        P.op("tensor", ws_tr, reads=["wsb", "ident"], writes=["ptw"])
        P.op("vector", lambda e: e.tensor_copy(wsT[:], ptw[:]), reads=["ptw"], writes=["wsT"])
        P.flush()

    tpn = [0]

    def transpose_phase(AT, src, src_dt, gT, norm):
        tpn[0] += 1
        u = "tp%d_" % tpn[0]
        with ExitStack() as s2:
            xs = [s2.enter_context(nc.sbuf_tensor(u + "xs%d" % i, [128, D], src_dt)) for i in range(2)]
            xb = [s2.enter_context(nc.sbuf_tensor(u + "xb%d" % i, [128, D], BF16)) for i in range(2)]
            junk = s2.enter_context(nc.sbuf_tensor(u + "junk", [128, D], BF16))
            st = s2.enter_context(nc.sbuf_tensor(u + "st", [128, 64], F32))
            pts = [s2.enter_context(nc.psum_tensor(u + "pt%d" % i, [128, 8, 128], BF16)) for i in range(4)]
            for tb in range(NTB):
                sl = tb % 2
                P.op("sync", lambda e, tb=tb, sl=sl: e.dma_start(out=xs[sl][:], in_=src[tb * 128:(tb + 1) * 128, :]),
                     writes=[("xs", sl)], dma=("tp_xs", sl))
                if norm:
                    P.chain("scalar", [
                        lambda e, tb=tb, sl=sl: e.activation(out=junk[:], in_=xs[sl][:], func=AF.Square,
                                                             accum_out=st[:, tb:tb + 1]),
                        lambda e, tb=tb: e.activation(out=st[:, 16 + tb:17 + tb], in_=st[:, tb:tb + 1], func=AF.Sqrt,
                                                      scale=1.0 / D, bias=epsc[:]),
                    ], reads=[("xs", sl), "epsc"], writes=[("st", tb)])
                    P.op("vector", lambda e, tb=tb: e.reciprocal(out=st[:, 32 + tb:33 + tb], in_=st[:, 16 + tb:17 + tb]),
                         reads=[("st", tb)], writes=[("rstd", tb)])
                    P.op("scalar", lambda e, tb=tb, sl=sl: e.activation(out=xb[sl][:], in_=xs[sl][:], func=AF.Copy,
                                                                      scale=st[:, 32 + tb:33 + tb]),
                         reads=[("xs", sl), ("rstd", tb)], writes=[("xb", sl)])
                    srcb = xb[sl]
                    rk = ("xb", sl)
                elif src_dt == BF16:
                    srcb = xs[sl]
                    rk = ("xs", sl)
                else:
                    P.op("scalar", lambda e, sl=sl: e.activation(out=xb[sl][:], in_=xs[sl][:], func=AF.Copy),
                         reads=[("xs", sl)], writes=[("xb", sl)])
                    srcb = xb[sl]
                    rk = ("xb", sl)
                for hf in range(2):
                    pi = (2 * tb + hf) % 4

                    def f_tr(e, hf=hf, pi=pi, srcb=srcb):
                        ins = None
                        for cc in range(8):
                            c = hf * 8 + cc
                            ins = e.transpose(pts[pi][:, cc, :], srcb[:, c * 128:(c + 1) * 128], ident[:])
                        return ins
                    P.op("tensor", f_tr, reads=[rk], writes=[("pt", pi)])
                    for cc in range(8):
                        c = hf * 8 + cc
                        dst = AT[:, c, tb * 128:(tb + 1) * 128]
                        if gT is None:
                            if cc % 2 == 0:
                                P.op("vector", lambda e, dst=dst, pi=pi, cc=cc: e.tensor_copy(dst, pts[pi][:, cc, :]),
                                     reads=[("pt", pi)], writes=[("AT", tb)])
                            else:
                                P.op("scalar", lambda e, dst=dst, pi=pi, cc=cc: e.activation(out=dst, in_=pts[pi][:, cc, :], func=AF.Copy),
                                     reads=[("pt", pi)], writes=[("AT", tb)])
                        else:
                            if cc % 2 == 0:
                                P.op("vector", lambda e, dst=dst, pi=pi, cc=cc, c=c: e.tensor_scalar(
                                    out=dst, in0=pts[pi][:, cc, :], scalar1=gT[:, c:c + 1], scalar2=None, op0=ALU.mult),
                                    reads=[("pt", pi)], writes=[("AT", tb)])
                            else:
                                P.op("scalar", lambda e, dst=dst, pi=pi, cc=cc, c=c: e.activation(
                                    out=dst, in_=pts[pi][:, cc, :], func=AF.Copy, scale=gT[:, c:c + 1]),
                                    reads=[("pt", pi)], writes=[("AT", tb)])
            P.flush()

    wcount = [0]

    def load_w(slot_tiles, wsrc_rows, ncols, kc, key):
        i = wcount[0] % len(slot_tiles)
        wcount[0] += 1
        t = slot_tiles[i]
        half = max(1, kc // 2)
        for a in range(0, kc, half):
            b = min(kc, a + half)
            P.op("gpsimd", lambda e, a=a, b=b, t=t: e.dma_start(
                out=t[:, a:b, 0:ncols], in_=wsrc_rows[a * 128:b * 128, :].rearrange("(k p) n -> p k n", p=128)),
                writes=[(key, i, a)], dma=(key, i, a))
        return t, [(key, i, a) for a in range(0, kc, half)]

    with nc.sbuf_tensor("aT", [128, 16, S], BF16) as aT:
      if not CFG["skipAB"]:
        transpose_phase(aT, x, F32, g1T, True)
        AT_keys = [("AT", tb) for tb in range(NTB)]

        with ExitStack() as s2:
            wsm = [s2.enter_context(nc.sbuf_tensor("wB_s%d" % i, [128, 16, 128], BF16)) for i in range(3)]
            wbg = [s2.enter_context(nc.sbuf_tensor("wB_b%d" % i, [128, 16, 512], BF16)) for i in range(3)]
            stg = [s2.enter_context(nc.sbuf_tensor("stB%d" % i, [128, 512], BF16)) for i in range(4)]
            gtmp = [s2.enter_context(nc.sbuf_tensor("gtB%d" % i, [128, 1024], F32)) for i in range(2)]
            junkB = s2.enter_context(nc.sbuf_tensor("junkB", [128, 1024], BF16))
            lnst = s2.enter_context(nc.sbuf_tensor("lnst", [128, 8 * NTB], F32))
            lngs = s2.enter_context(nc.sbuf_tensor("lngs", [128, 1024], F32))
            lnbs = s2.enter_context(nc.sbuf_tensor("lnbs", [128, 1024], F32))
            vnst = [s2.enter_context(nc.sbuf_tensor("vnst%d" % i, [128, 1024], BF16)) for i in range(2)]
            pbs = [s2.enter_context(nc.psum_tensor("pB%d" % i, [128, 512], F32)) for i in range(6)]
            cdma(lngs[:], lngb, "lngs")
            cdma(lnbs[:], lnbb, "lnbs")
            pcount = 0
            scount = 0
            for cch in range(16):
                wt, wk = load_w(wsm, w_in[:, cch * 128:(cch + 1) * 128], 128, 16, "wsm")
                for tc4 in range(4):
                    pi = pcount % 6
                    pcount += 1

                    def f_mm(e, wt=wt, tc4=tc4, pi=pi):
                        ins = None
                        for k in range(16):
                            ins = e.matmul(pbs[pi][:], wt[:, k, :], aT[:, k, tc4 * 512:(tc4 + 1) * 512],
                                           start=(k == 0), stop=(k == 15))
                        return ins
                    P.op("tensor", f_mm, reads=wk + AT_keys, writes=[("pb", pi)])
                    si = scount % 4
                    scount += 1
                    if tc4 % 2 == 0:
                        P.op("scalar", lambda e, si=si, pi=pi: e.activation(out=stg[si][:], in_=pbs[pi][:], func=AF.Copy),
                             reads=[("pb", pi)], writes=[("stg", si)])
                    else:
                        P.op("vector", lambda e, si=si, pi=pi: e.tensor_copy(stg[si][:], pbs[pi][:]),
                             reads=[("pb", pi)], writes=[("stg", si)])
                    P.op("sync", lambda e, si=si, cch=cch, tc4=tc4: e.dma_start(
                        out=QK[cch, :, tc4 * 512:(tc4 + 1) * 512], in_=stg[si][:]),
                        reads=[("stg", si)], dma=("stg", si))
            for which, (dst, col0) in enumerate(((VS, 2048), (US, 3072))):
                for cc in range(2):
                    wt, wk = load_w(wbg, w_in[:, col0 + cc * 512: col0 + (cc + 1) * 512], 512, 16, "wbg")
                    for tb in range(NTB):
                        pi = pcount % 6
                        pcount += 1

                        def f_mm(e, wt=wt, tb=tb, pi=pi):
                            ins = None
                            for k in range(16):
                                ins = e.matmul(pbs[pi][:], aT[:, k, tb * 128:(tb + 1) * 128], wt[:, k, :],
                                               start=(k == 0), stop=(k == 15))
                            return ins
                        P.op("tensor", f_mm, reads=wk + AT_keys, writes=[("pb", pi)])
                        si = scount % 4
                        scount += 1
                        if which == 0:
                            P.op("vector", lambda e, si=si, pi=pi: e.tensor_copy(stg[si][:], pbs[pi][:]),
                                 reads=[("pb", pi)], writes=[("stg", si)])
                        else:
                            P.op("scalar", lambda e, si=si, pi=pi: e.activation(out=stg[si][:], in_=pbs[pi][:],
                                                                              func=AF.Gelu_apprx_tanh),
                                 reads=[("pb", pi)], writes=[("stg", si)])
                        P.op("sync", lambda e, si=si, dst=dst, tb=tb, cc=cc: e.dma_start(
                            out=dst[tb * 128:(tb + 1) * 128, cc * 512:(cc + 1) * 512], in_=stg[si][:]),
                            reads=[("stg", si)], dma=("stg", si))
            wt0, wk0 = load_w(wbg, w_in[:, 4096:4608], 512, 16, "wbg")
            wt1, wk1 = load_w(wbg, w_in[:, 4608:5120], 512, 16, "wbg")
            for tb in range(NTB):
                gi = tb % 2
                pis = []
                for cc, (wt, wk) in enumerate(((wt0, wk0), (wt1, wk1))):
                    pi = pcount % 6
                    pcount += 1
                    pis.append(pi)

                    def f_mm(e, wt=wt, tb=tb, pi=pi):
                        ins = None
                        for k in range(16):
                            ins = e.matmul(pbs[pi][:], aT[:, k, tb * 128:(tb + 1) * 128], wt[:, k, :],
                                           start=(k == 0), stop=(k == 15))
                        return ins
                    P.op("tensor", f_mm, reads=wk + AT_keys, writes=[("pb", pi)])

                o = 8 * tb
                p0, p1 = pis
                P.chain("scalar", [
                    lambda e, gi=gi, p0=p0, o=o: e.activation(out=gtmp[gi][:, 0:512], in_=pbs[p0][:], func=AF.Gelu_apprx_tanh,
                                                             accum_out=lnst[:, o:o + 1]),
                    lambda e, gi=gi, p1=p1, o=o: e.activation(out=gtmp[gi][:, 512:1024], in_=pbs[p1][:], func=AF.Gelu_apprx_tanh,
                                                             accum_out=lnst[:, o + 1:o + 2]),
                    lambda e, gi=gi, o=o: e.activation(out=junkB[:], in_=gtmp[gi][:], func=AF.Square,
                                                      accum_out=lnst[:, o + 2:o + 3]),
                ], reads=[("pb", p0), ("pb", p1)], writes=[("gtmp", gi), ("lnst", tb)])
                P.chain("vector", [
                    lambda e, o=o: e.tensor_tensor(out=lnst[:, o + 3:o + 4], in0=lnst[:, o:o + 1], in1=lnst[:, o + 1:o + 2], op=ALU.add),
                    lambda e, o=o: e.tensor_scalar(out=lnst[:, o + 3:o + 4], in0=lnst[:, o + 3:o + 4], scalar1=1.0 / 1024,
                                                   scalar2=None, op0=ALU.mult),
                ], reads=[("lnst", tb)], writes=[("lnm", tb)])
                P.op("scalar", lambda e, o=o: e.activation(out=lnst[:, o + 4:o + 5], in_=lnst[:, o + 3:o + 4], func=AF.Square),
                     reads=[("lnm", tb)], writes=[("lnmsq", tb)])
                P.chain("vector", [
                    lambda e, o=o: e.tensor_scalar(out=lnst[:, o + 5:o + 6], in0=lnst[:, o + 2:o + 3], scalar1=1.0 / 1024,
                                                   scalar2=None, op0=ALU.mult),
                    lambda e, o=o: e.tensor_tensor(out=lnst[:, o + 5:o + 6], in0=lnst[:, o + 5:o + 6], in1=lnst[:, o + 4:o + 5],
                                                   op=ALU.subtract),
                ], reads=[("lnmsq", tb), ("lnst", tb)], writes=[("lnvar", tb)])
                P.op("scalar", lambda e, o=o: e.activation(out=lnst[:, o + 6:o + 7], in_=lnst[:, o + 5:o + 6],
                                                          func=AF.Sqrt, bias=epsc[:]),
                     reads=[("lnvar", tb)], writes=[("lnsd", tb)])
                P.chain("vector", [
                    lambda e, o=o: e.reciprocal(out=lnst[:, o + 7:o + 8], in_=lnst[:, o + 6:o + 7]),
                    lambda e, o=o, gi=gi: e.tensor_scalar(out=gtmp[gi][:], in0=gtmp[gi][:], scalar1=lnst[:, o + 3:o + 4],
                                                          scalar2=None, op0=ALU.subtract),
                    lambda e, o=o, gi=gi: e.tensor_scalar(out=gtmp[gi][:], in0=gtmp[gi][:], scalar1=lnst[:, o + 7:o + 8],
                                                          scalar2=None, op0=ALU.mult),
                    lambda e, gi=gi: e.tensor_tensor(out=gtmp[gi][:], in0=gtmp[gi][:], in1=lngs[:], op=ALU.mult),
                    lambda e, gi=gi: e.tensor_tensor(out=vnst[gi][:], in0=gtmp[gi][:], in1=lnbs[:], op=ALU.add),
                ], reads=[("lnsd", tb), ("lnm", tb), ("gtmp", gi), "lngs", "lnbs"], writes=[("gtmp", gi), ("vnst", gi)])
                P.op("sync", lambda e, tb=tb, gi=gi: e.dma_start(out=VN[tb * 128:(tb + 1) * 128, :], in_=vnst[gi][:]),
                     reads=[("vnst", gi)], dma=("vnst", gi))
            if dbg:
                P.op("sync", lambda e: e.dma_start(out=DBG[:, 0:128], in_=lnst[:]), reads=[("lnsd", t) for t in range(NTB)], dma="dbg0")
                P.op("sync", lambda e: e.dma_start(out=DBG[:, 128:129], in_=neglam[:], allow_slow_non_contiguous=True), dma="dbg1")
                P.op("sync", lambda e: e.dma_start(out=DBG[:, 256:1280], in_=EB[:].rearrange("p a b -> p (a b)")), dma="dbg2")
            P.flush()
    if stop_after == "B":
        es.close()
        return nc

    SCALE = 128.0 ** -0.5
    with ExitStack() as s2:
        sbt = lambda name, shape, dt=F32: s2.enter_context(nc.sbuf_tensor(name, shape, dt))
        QTh = [sbt("QTh%d" % i, [128, 2, S], BF16) for i in range(2)]
        KTh = [sbt("KTh%d" % i, [128, 2, S], BF16) for i in range(2)]
        Vh = [sbt("Vh%d" % i, [128, 16, 264], BF16) for i in range(2)]
        PT = [sbt("PT%d" % i, [128, 4, 128], BF16) for i in range(4)]
        sublnS = sbt("sublnS", [128, 256])
        t1 = [sbt("t1_%d" % i, [128, 256]) for i in range(2)]
        ob = [sbt("ob_%d" % i, [128, 256]) for i in range(2)]
        o2 = [sbt("o2_%d" % i, [128, 256]) for i in range(2)]
        od = [sbt("od_%d" % i, [128, 256], BF16) for i in range(2)]
        junkC = sbt("junkC", [128, 256], BF16)
        rs = sbt("rs", [128, 8 * 64])
        Sb = [s2.enter_context(nc.psum_tensor("Sb%d" % i, [128, 512], F32)) for i in range(3)]
        po = [s2.enter_context(nc.psum_tensor("po%d" % i, [128, 512], F32)) for i in range(4)]
        cdma(sublnS[:], sublnb, "sublnS")
        P.op("vector", lambda e: e.tensor_scalar(out=sublnS[:], in0=sublnS[:], scalar1=0.8, scalar2=None, op0=ALU.mult),
             reads=["sublnS"], writes=["sublnS"])
        for i in range(2):
            if CFG["ones"]:
                P.op("gpsimd", lambda e, i=i: e.memset(Vh[i][:, :, 256:264], 1.0), writes=[("Vh1", i)])
        for h in range(CFG["heads"]):
            hb = h % 2
            for m in range(2):
                P.op("sync", lambda e, hb=hb, m=m, h=h: e.dma_start(out=QTh[hb][:, m, :], in_=QK[2 * h + m]),
                     writes=[("QTh", hb, m)], dma=("QTh", hb, m))
                P.op("sync", lambda e, hb=hb, m=m, h=h: e.dma_start(out=KTh[hb][:, m, :], in_=QK[8 + 2 * h + m]),
                     writes=[("KTh", hb, m)], dma=("KTh", hb, m))
            P.op("sync", lambda e, hb=hb, h=h: e.dma_start(
                out=Vh[hb][:, :, 0:256], in_=VS[:, h * 256:(h + 1) * 256].rearrange("(b p) e -> p b e", p=128)),
                writes=[("Vh", hb)], dma=("Vh", hb))
            units = [(b, m, g0) for b in range(CFG["nb"]) for m in range(2) for g0 in range(0, b + 1, 4)]

            def rec_S(u, ui, h=h, hb=hb):
                b, m, g0 = u
                js = list(range(g0, min(g0 + 4, b + 1)))
                n = len(js)
                si, pi = ui % 3, ui % 4

                def f_s(e):
                    ins = None
                    for jj, j in enumerate(js):
                        ins = e.matmul(Sb[si][:, jj * 128:(jj + 1) * 128], KTh[hb][:, m, j * 128:(j + 1) * 128],
                                       QTh[hb][:, m, b * 128:(b + 1) * 128], start=True, stop=True)
                    return ins
                P.op("tensor", f_s, reads=[("QTh", hb, m), ("KTh", hb, m)], writes=[("Sb", si)])
                P.op("scalar", lambda e: e.activation(out=PT[pi][:, 0:n, :].rearrange("p a b -> p (a b)"),
                                                      in_=Sb[si][:, 0:n * 128], func=AF.Exp, scale=SCALE),
                     reads=[("Sb", si)], writes=[("PT", pi)])
                for jj, j in enumerate(js):
                    if j == b:
                        w = 0
                    elif j == b - 1:
                        w = 1
                    else:
                        continue
                    if not CFG["near"]:
                        continue
                    P.op("vector", lambda e, jj=jj, w=w: e.tensor_tensor(out=PT[pi][:, jj, :], in0=PT[pi][:, jj, :],
                                                                         in1=EB[:, 2 * h + w, :], op=ALU.mult),
                         reads=[("PT", pi), "EB"], writes=[("PT", pi)])

            def rec_PV(u, ui, h=h, hb=hb):
                b, m, g0 = u
                js = list(range(g0, min(g0 + 4, b + 1)))
                pi = ui % 4
                pidx = (2 * b + m) % 4

                def f_pv(e):
                    ins = None
                    for jj, j in enumerate(js):
                        ins = e.matmul(po[pidx][:, 0:258], PT[pi][:, jj, :], Vh[hb][:, j, 0:258],
                                       start=(j == 0), stop=(j == b))
                    return ins
                if CFG["pv"]:
                    P.op("tensor", f_pv, reads=[("PT", pi), ("Vh", hb), ("Vh1", hb)], writes=[("po", pidx)])

            def rec_comb(b, h=h):
                c = 8 * (h * 16 + b)
                ti = b % 2
                i0, i1 = (2 * b) % 4, (2 * b + 1) % 4
                p0, p1 = po[i0], po[i1]
                col = lambda k: rs[:, c + k:c + k + 1]
                lv = CFG["comb"]
                if lv < 1:
                    return
                P.chain("vector", [
                    lambda e: e.reciprocal(out=col(0), in_=p0[:, 256:257]),
                    lambda e: e.reciprocal(out=col(1), in_=p1[:, 256:257]),
                    lambda e: e.tensor_scalar(out=col(2), in0=col(1), scalar1=neglam[:, 0:1], scalar2=None, op0=ALU.mult),
                ], reads=[("po", i0), ("po", i1), "neglam"], writes=[("rs", c)])
                if lv < 2:
                    return
                P.op("scalar", lambda e: e.activation(out=t1[ti][:], in_=p1[:, 0:256], func=AF.Copy, scale=col(2)),
                     reads=[("rs", c), ("po", i1)], writes=[("t1", ti)])
                if lv < 3:
                    return
                P.chain("vector", [
                    lambda e: e.tensor_scalar(out=ob[ti][:], in0=p0[:, 0:256], scalar1=col(0), scalar2=None, op0=ALU.mult),
                    lambda e: e.tensor_tensor(out=ob[ti][:], in0=ob[ti][:], in1=t1[ti][:], op=ALU.add),
                ], reads=[("rs", c), ("po", i0), ("t1", ti)], writes=[("ob", ti)])
                if lv < 4:
                    return
                P.chain("scalar", [
                    lambda e: e.activation(out=junkC[:], in_=ob[ti][:], func=AF.Square, accum_out=col(3)),
                    lambda e: e.activation(out=col(4), in_=col(3), func=AF.Sqrt, scale=1.0 / 256, bias=epsc[:]),
                ], reads=[("ob", ti), "epsc"], writes=[("rs2", c)])
                if lv < 5:
                    return
                P.op("vector", lambda e: e.reciprocal(out=col(5), in_=col(4)), reads=[("rs2", c)], writes=[("rs3", c)])
                P.op("scalar", lambda e: e.activation(out=o2[ti][:], in_=ob[ti][:], func=AF.Copy, scale=col(5)),
                     reads=[("rs3", c), ("ob", ti)], writes=[("o2", ti)])
                P.op("vector", lambda e: e.tensor_tensor(out=od[ti][:], in0=o2[ti][:], in1=sublnS[:], op=ALU.mult),
                     reads=[("o2", ti), "sublnS"], writes=[("od", ti)])
                P.op("sync", lambda e: e.dma_start(out=MX[b * 128:(b + 1) * 128, h * 256:(h + 1) * 256], in_=od[ti][:]),
                     reads=[("od", ti)], dma=("od", ti))

            for i in range(len(units) + 1):
                if i < len(units):
                    rec_S(units[i], i)
                if i >= 1:
                    u = units[i - 1]
                    rec_PV(u, i - 1)
                    if u[1] == 1 and u[2] + 4 > u[0]:
                        rec_comb(u[0])
        P.flush()
    if stop_after == "C1":
        es.close()
        return nc

    with ExitStack() as s2:
        sbt = lambda name, shape, dt=F32: s2.enter_context(nc.sbuf_tensor(name, shape, dt))
        vnT = [sbt("vnT%d" % i, [128, 1024], BF16) for i in range(2)]
        uT = [sbt("uT%d" % i, [128, 1024], BF16) for i in range(2)]
        tmpg = [sbt("tmpg%d" % i, [128, 1024]) for i in range(2)]
        og = [sbt("og%d" % i, [128, 1024], BF16) for i in range(2)]
        pgb = [s2.enter_context(nc.psum_tensor("pg%d" % i, [128, 512], F32)) for i in range(4)]
        pgs = lambda sl, g: pgb[2 * sl + g // 4][:, (g % 4) * 128:(g % 4 + 1) * 128]
        for tb in range(NTB):
            sl = tb % 2
            P.op("sync", lambda e, tb=tb, sl=sl: e.dma_start(out=vnT[sl][:], in_=VN[tb * 128:(tb + 1) * 128, :]),
                 writes=[("vnT", sl)], dma=("vnT", sl))
            P.op("sync", lambda e, tb=tb, sl=sl: e.dma_start(out=uT[sl][:], in_=US[tb * 128:(tb + 1) * 128, :]),
                 writes=[("uT", sl)], dma=("uT", sl))

            def f_sp(e, sl=sl):
                ins = None
                for g in range(8):
                    ins = e.matmul(pgs(sl, g), wsT[:, g, :], vnT[sl][:, g * 128:(g + 1) * 128],
                                   start=True, stop=True)
                return ins
            P.op("tensor", f_sp, reads=[("vnT", sl), "wsT"], writes=[("pg", sl)])

            def f_ba(e, sl=sl):
                ins = None
                for g in range(1, 8, 2):
                    ins = e.activation(out=tmpg[sl][:, g * 128:(g + 1) * 128], in_=pgs(sl, g),
                                       func=AF.Identity, bias=bsT[:, g:g + 1])
                return ins

            def f_bd(e, sl=sl):
                ins = None
                for g in range(8):
                    ins = e.tensor_scalar(out=tmpg[sl][:, g * 128:(g + 1) * 128], in0=pgs(sl, g),
                                          scalar1=bsT[:, g:g + 1], scalar2=None, op0=ALU.add)
                return ins
            P.op("vector", f_bd, reads=[("pg", sl), "bsT"], writes=[("tmpgD", sl)])
            P.op("vector", lambda e, sl=sl: e.tensor_tensor(out=og[sl][:], in0=tmpg[sl][:], in1=uT[sl][:], op=ALU.mult),
                 reads=[("tmpgD", sl), ("uT", sl)], writes=[("og", sl), ("tmpgD", sl)])
            P.op("sync", lambda e, tb=tb, sl=sl: e.dma_start(out=MX[tb * 128:(tb + 1) * 128, 1024:2048], in_=og[sl][:]),
                 reads=[("og", sl)], dma=("og", sl))
        P.flush()
    if stop_after == "C":
        es.close()
        return nc

    with nc.sbuf_tensor("mT", [128, 16, S], BF16) as mT:
        transpose_phase(mT, MX, BF16, None, False)
        AT_keys = [("AT", tb) for tb in range(NTB)]
        with ExitStack() as s2:
            sbt = lambda name, shape, dt=F32: s2.enter_context(nc.sbuf_tensor(name, shape, dt))
            wo = [sbt("wo%d" % i, [128, 16, 512], BF16) for i in range(4)]
            xt = [sbt("xtD%d" % i, [128, D]) for i in range(2)]
            ht = [sbt("htD%d" % i, [128, D]) for i in range(2)]
            pD = [s2.enter_context(nc.psum_tensor("pD%d" % i, [128, 512], F32)) for i in range(4)]
            wts = [load_w(wo, w_out[:, cc * 512:(cc + 1) * 512], 512, 16, "wo") for cc in range(4)]
            for tb in range(NTB):
                sl = tb % 2
                P.op("sync", lambda e, tb=tb, sl=sl: e.dma_start(out=xt[sl][:], in_=x[tb * 128:(tb + 1) * 128, :]),
                     writes=[("xt", sl)], dma=("xtD", sl))
                for cc in range(4):
                    pi = (tb * 4 + cc) % 4
                    wt, wk = wts[cc]

                    def f_mm(e, wt=wt, tb=tb, pi=pi):
                        ins = None
                        for k in range(16):
                            ins = e.matmul(pD[pi][:], mT[:, k, tb * 128:(tb + 1) * 128], wt[:, k, :],
                                           start=(k == 0), stop=(k == 15))
                        return ins
                    P.op("tensor", f_mm, reads=wk, writes=[("pD", pi)])
                    P.op("vector", lambda e, sl=sl, cc=cc, pi=pi: e.tensor_tensor(
                        out=ht[sl][:, cc * 512:(cc + 1) * 512], in0=pD[pi][:], in1=xt[sl][:, cc * 512:(cc + 1) * 512], op=ALU.add),
                        reads=[("pD", pi), ("xt", sl)], writes=[("ht", sl)])
                P.op("sync", lambda e, tb=tb, sl=sl: e.dma_start(out=H1[tb * 128:(tb + 1) * 128, :], in_=ht[sl][:]),
                     reads=[("ht", sl)], dma=("htD", sl))
            P.flush()
    if stop_after == "D":
        es.close()
        return nc

    with nc.sbuf_tensor("fT", [128, 16, S], BF16) as fT:
        transpose_phase(fT, H1, F32, g2T, True)
        with ExitStack() as s2:
            sbt = lambda name, shape, dt=F32: s2.enter_context(nc.sbuf_tensor(name, shape, dt))
            wa = [sbt("wa%d" % i, [128, 16, 128], BF16) for i in range(3)]
            wb3 = [sbt("wb3_%d" % i, [128, 16, 128], BF16) for i in range(3)]
            sg = [sbt("sgE%d" % i, [128, 512]) for i in range(2)]
            gg = [sbt("ggE%d" % i, [128, 512], BF16) for i in range(3)]
            pA = [s2.enter_context(nc.psum_tensor("pA%d" % i, [128, 512], F32)) for i in range(3)]
            pB = [s2.enter_context(nc.psum_tensor("pB3_%d" % i, [128, 512], F32)) for i in range(3)]
            for c in range(NHC):
                wt1, wk1 = load_w(wa, w1[:, c * 128:(c + 1) * 128], 128, 16, "wa")
                wt3, wk3 = load_w(wb3, w3[:, c * 128:(c + 1) * 128], 128, 16, "wb3")
                for tc4 in range(4):
                    i = c * 4 + tc4
                    pi, si, gi = i % 3, i % 2, i % 3

                    def f_mm(e, wt1=wt1, wt3=wt3, tc4=tc4, pi=pi):
                        ins = None
                        for k in range(16):
                            ins = e.matmul(pA[pi][:], wt1[:, k, :], fT[:, k, tc4 * 512:(tc4 + 1) * 512],
                                           start=(k == 0), stop=(k == 15))
                        for k in range(16):
                            ins = e.matmul(pB[pi][:], wt3[:, k, :], fT[:, k, tc4 * 512:(tc4 + 1) * 512],
                                           start=(k == 0), stop=(k == 15))
                        return ins
                    P.op("tensor", f_mm, reads=wk1 + wk3, writes=[("pA", pi), ("pB", pi)])
                    P.op("scalar", lambda e, si=si, pi=pi: e.activation(out=sg[si][:], in_=pA[pi][:], func=AF.Silu),
                         reads=[("pA", pi)], writes=[("sg", si)])
                    P.op("vector", lambda e, si=si, pi=pi, gi=gi: e.tensor_tensor(out=gg[gi][:], in0=sg[si][:], in1=pB[pi][:],
                                                                                 op=ALU.mult),
                         reads=[("sg", si), ("pB", pi)], writes=[("gg", gi)])
                    P.op("sync", lambda e, c=c, tc4=tc4, gi=gi: e.dma_start(out=GT[c, :, tc4 * 512:(tc4 + 1) * 512], in_=gg[gi][:]),
                         reads=[("gg", gi)], dma=("ggE", gi))
            P.flush()
    with ExitStack() as s2:
        sbt = lambda name, shape, dt=F32: s2.enter_context(nc.sbuf_tensor(name, shape, dt))
        w2p = [sbt("w2p%d" % i, [128, NHC, 512], BF16) for i in range(2)]
        gt = [sbt("gtE%d" % i, [128, NHC, 128], BF16) for i in range(2)]
        h1p = [sbt("h1p%d" % i, [128, 512]) for i in range(3)]
        h2p = [sbt("h2p%d" % i, [128, 512]) for i in range(3)]
        pE = [s2.enter_context(nc.psum_tensor("pE%d" % i, [128, 512], F32)) for i in range(3)]
        for cc in range(4):
            wt, wk = load_w(w2p, w2[:, cc * 512:(cc + 1) * 512], 512, NHC, "w2p")
            for tb in range(NTB):
                i = cc * NTB + tb
                sl, pi, hi = i % 2, i % 3, i % 3
                for hf in range(2):
                    P.op("sync", lambda e, tb=tb, sl=sl, hf=hf: e.dma_start(
                        out=gt[sl][:, hf * 22:(hf + 1) * 22, :],
                        in_=GT[hf * 22:(hf + 1) * 22, :, tb * 128:(tb + 1) * 128].rearrange("c p t -> p c t")),
                        writes=[("gt", sl, hf)], dma=("gtE", sl, hf))
                P.op("sync", lambda e, tb=tb, cc=cc, hi=hi: e.dma_start(
                    out=h1p[hi][:], in_=H1[tb * 128:(tb + 1) * 128, cc * 512:(cc + 1) * 512]),
                    writes=[("h1p", hi)], dma=("h1p", hi))

                def f_mm(e, wt=wt, sl=sl, pi=pi):
                    ins = None
                    for c in range(NHC):
                        ins = e.matmul(pE[pi][:], gt[sl][:, c, :], wt[:, c, :], start=(c == 0), stop=(c == NHC - 1))
                    return ins
                P.op("tensor", f_mm, reads=wk + [("gt", sl, 0), ("gt", sl, 1)], writes=[("pE", pi)])
                P.op("vector", lambda e, pi=pi, hi=hi: e.tensor_tensor(out=h2p[hi][:], in0=pE[pi][:], in1=h1p[hi][:], op=ALU.add),
                     reads=[("pE", pi), ("h1p", hi)], writes=[("h2p", hi)])
                P.op("sync", lambda e, tb=tb, cc=cc, hi=hi: e.dma_start(
                    out=H2[tb * 128:(tb + 1) * 128, cc * 512:(cc + 1) * 512], in_=h2p[hi][:]),
                    reads=[("h2p", hi)], dma=("h2p", hi))
        P.flush()
    if stop_after == "E":
        es.close()
        return nc

    with nc.sbuf_tensor("hT", [128, 16, S], BF16) as hT, nc.sbuf_tensor("pT", [128, 2, S], BF16) as pT:
        transpose_phase(hT, H2, F32, None, False)
        with ExitStack() as s2:
            sbt = lambda name, shape, dt=F32: s2.enter_context(nc.sbuf_tensor(name, shape, dt))
            pf = sbt("pf", [128, NTB, 256])
            pb = sbt("pb", [128, NTB, 256], BF16)
            ptp = [s2.enter_context(nc.psum_tensor("ptp%d" % i, [128, 8, 128], BF16)) for i in range(2)]
            wg = [sbt("wg%d" % i, [128, 16, 512], BF16) for i in range(2)]
            wu = [sbt("wu%d" % i, [128, 2, 512], BF16) for i in range(2)]
            h2q = [sbt("h2q%d" % i, [128, 512]) for i in range(3)]
            sgt = [sbt("sgt%d" % i, [128, 512]) for i in range(2)]
            tt = [sbt("ttF%d" % i, [128, 512]) for i in range(2)]
            h3p = [sbt("h3p%d" % i, [128, 512]) for i in range(3)]
            pG = [s2.enter_context(nc.psum_tensor("pG%d" % i, [128, 512], F32)) for i in range(2)]
            pU = [s2.enter_context(nc.psum_tensor("pU%d" % i, [128, 512], F32)) for i in range(2)]
            P.op("sync", lambda e: e.dma_start(out=pf[:], in_=p_in.rearrange("(b p) e -> p b e", p=128)),
                 writes=["pf"], dma="pf")
            P.op("scalar", lambda e: e.activation(out=pb[:].rearrange("p a b -> p (a b)"),
                                                  in_=pf[:].rearrange("p a b -> p (a b)"), func=AF.Copy),
                 reads=["pf"], writes=["pb"])
            for tb in range(NTB):
                qi = tb % 2

                def f_tr(e, tb=tb, qi=qi):
                    ins = None
                    for k2 in range(2):
                        ins = e.transpose(ptp[qi][:, k2, :], pb[:, tb, k2 * 128:(k2 + 1) * 128], ident[:])
                    return ins
                P.op("tensor", f_tr, reads=["pb"], writes=[("ptp", qi)])
                P.op("vector", lambda e, tb=tb, qi=qi: e.tensor_copy(pT[:, 0:2, tb * 128:(tb + 1) * 128], ptp[qi][:, 0:2, :]),
                     reads=[("ptp", qi)], writes=[("pT", tb)])
            pT_keys = [("pT", tb) for tb in range(NTB)]
            for cc in range(4):
                wtg, wkg = load_w(wg, w_gate[:, cc * 512:(cc + 1) * 512], 512, 16, "wg")
                wtu, wku = load_w(wu, w_up[:, cc * 512:(cc + 1) * 512], 512, 2, "wu")
                for tb in range(NTB):
                    i = cc * NTB + tb
                    pi, hi, si = i % 2, i % 3, i % 2
                    P.op("sync", lambda e, tb=tb, cc=cc, hi=hi: e.dma_start(
                        out=h2q[hi][:], in_=H2[tb * 128:(tb + 1) * 128, cc * 512:(cc + 1) * 512]),
                        writes=[("h2q", hi)], dma=("h2q", hi))

                    def f_mm(e, wtg=wtg, wtu=wtu, tb=tb, pi=pi):
                        ins = None
                        for k in range(16):
                            ins = e.matmul(pG[pi][:], hT[:, k, tb * 128:(tb + 1) * 128], wtg[:, k, :],
                                           start=(k == 0), stop=(k == 15))
                        for k in range(2):
                            ins = e.matmul(pU[pi][:], pT[:, k, tb * 128:(tb + 1) * 128], wtu[:, k, :],
                                           start=(k == 0), stop=(k == 1))
                        return ins
                    P.op("tensor", f_mm, reads=wkg + wku + pT_keys, writes=[("pG", pi), ("pU", pi)])
                    P.op("scalar", lambda e, si=si, pi=pi: e.activation(out=sgt[si][:], in_=pG[pi][:], func=AF.Sigmoid),
                         reads=[("pG", pi)], writes=[("sgt", si)])
                    P.chain("vector", [
                        lambda e, si=si, pi=pi: e.tensor_tensor(out=tt[si][:], in0=sgt[si][:], in1=pU[pi][:], op=ALU.mult),
                        lambda e, si=si, hi=hi: e.tensor_tensor(out=h3p[hi][:], in0=tt[si][:], in1=h2q[hi][:], op=ALU.add),
                    ], reads=[("sgt", si), ("pU", pi), ("h2q", hi)], writes=[("tt", si), ("h3p", hi)])
                    P.op("sync", lambda e, tb=tb, cc=cc, hi=hi: e.dma_start(
                        out=H1[tb * 128:(tb + 1) * 128, cc * 512:(cc + 1) * 512], in_=h3p[hi][:]),
                        reads=[("h3p", hi)], dma=("h3p", hi))
            P.flush()

    with ExitStack() as s2:
        sbt = lambda name, shape, dt=F32: s2.enter_context(nc.sbuf_tensor(name, shape, dt))
        h3t = [sbt("h3t%d" % i, [128, D]) for i in range(2)]
        o1 = [sbt("o1_%d" % i, [128, D]) for i in range(2)]
        ot = [sbt("ot_%d" % i, [128, D]) for i in range(2)]
        gfS = sbt("gfS", [128, D])
        junkG = sbt("junkG", [128, D], BF16)
        stf = sbt("stf", [128, 64])
        cdma(gfS[:], gfb, "gfS")
        for tb in range(NTB):
            sl = tb % 2
            P.op("sync", lambda e, tb=tb, sl=sl: e.dma_start(out=h3t[sl][:], in_=H1[tb * 128:(tb + 1) * 128, :]),
                 writes=[("h3t", sl)], dma=("h3t", sl))
            P.chain("scalar", [
                lambda e, tb=tb, sl=sl: e.activation(out=junkG[:], in_=h3t[sl][:], func=AF.Square, accum_out=stf[:, tb:tb + 1]),
                lambda e, tb=tb: e.activation(out=stf[:, 16 + tb:17 + tb], in_=stf[:, tb:tb + 1], func=AF.Sqrt,
                                              scale=1.0 / D, bias=epsc[:]),
            ], reads=[("h3t", sl), "epsc"], writes=[("stf", tb)])
            P.op("vector", lambda e, tb=tb: e.reciprocal(out=stf[:, 32 + tb:33 + tb], in_=stf[:, 16 + tb:17 + tb]),
                 reads=[("stf", tb)], writes=[("stf2", tb)])
            P.op("scalar", lambda e, tb=tb, sl=sl: e.activation(out=o1[sl][:], in_=h3t[sl][:], func=AF.Copy,
                                                              scale=stf[:, 32 + tb:33 + tb]),
                 reads=[("stf2", tb), ("h3t", sl)], writes=[("o1", sl)])
            P.op("vector", lambda e, sl=sl: e.tensor_tensor(out=ot[sl][:], in0=o1[sl][:], in1=gfS[:], op=ALU.mult),
                 reads=[("o1", sl), "gfS"], writes=[("ot", sl)])
            P.op("sync", lambda e, tb=tb, sl=sl: e.dma_start(out=out[tb * 128:(tb + 1) * 128, :], in_=ot[sl][:]),
                 reads=[("ot", sl)], dma=("ot", sl))
        P.flush()
    es.close()
    return nc


def prep(inputs):
    f = lambda a: np.ascontiguousarray(np.asarray(a, dtype=np.float32))
    x = f(inputs["x"]); p = f(inputs["p"])[0]
    rb = f(inputs["rel_bias"])
    kk = np.arange(128)[:, None]; qq = np.arange(128)[None, :]
    idx_d = t5_bucket_np(kk - qq); idx_p = t5_bucket_np(kk - qq - 128)
    bn = np.zeros((128, 4, 2, 128), np.float32)
    for h in range(4):
        bn[:, h, 0, :] = rb[idx_d, h]
        bn[:, h, 1, :] = rb[idx_p, h]
    mk = np.ones((128, 128), np.float32); mk[64:, :64] = 0.0
    bc = lambda v, n: np.ascontiguousarray(np.broadcast_to(f(v).reshape(1, -1), (128, n)))
    lam4 = np.concatenate([f(inputs[k]).reshape(-1) for k in ("lambda_q1", "lambda_k1", "lambda_q2", "lambda_k2")])
    shared = {
        "w_in": f(inputs["w_in"])[0], "w_out": f(inputs["w_out"])[0],
        "ffn_w1": f(inputs["ffn_w1"])[0], "ffn_w3": f(inputs["ffn_w3"])[0], "ffn_w2": f(inputs["ffn_w2"])[0],
        "pl_w_up": f(inputs["pl_w_up"])[0], "pl_w_gate": f(inputs["pl_w_gate"])[0],
        "g1T": np.ascontiguousarray(f(inputs["attn_norm_g"])[0].reshape(16, 128).T),
        "g2T": np.ascontiguousarray(f(inputs["ffn_norm_g"])[0].reshape(16, 128).T),
        "gfB": bc(inputs["final_norm_g"], 2048),
        "lam4": bc(lam4, 512), "sublnB": bc(inputs["subln_g"], 256),
        "lngB": bc(inputs["gmlp_ln_g"], 1024), "lnbB": bc(inputs["gmlp_ln_b"], 1024),
        "ws": f(inputs["gmlp_ws"])[0], "bsT": np.ascontiguousarray(f(inputs["gmlp_b"])[0].T),
        "bnear": bn.reshape(128, 1024), "r15": bc(rb[15], 4), "mk": mk,
    }
    return [dict(shared, x=x[b], p=p[b]) for b in range(8)]


def kernel(**inputs):
    maps = prep(inputs)
    nc = build()
    outs = []
    for half in range(2):
        res = run_bass_kernel_spmd(nc, maps[4 * half:4 * half + 4], core_ids=list(range(4)))
        outs.extend(np.asarray(r["out"]) for r in res.results)
    return np.stack(outs, axis=0).astype(np.float32)
```
